# Optimizing a Trainium2 kernel written in Bass

```python
import jax, jax.numpy as jnp
from jax import lax
import numpy as np

D_MODEL = 2048
BATCH = 1
SEQ = 16384
DEPTH = 1

CHUNK = 64
EPS = 1e-6
HG_HEADS = 8
HG_KDIM = 128
HG_VDIM = 128
HG_KW = HG_HEADS * HG_KDIM
HG_VW = HG_HEADS * HG_VDIM
AT_HEADS = 8
AT_HDIM = 128
AT_W = AT_HEADS * AT_HDIM
BAND_CHUNKS = 9
REL_CLIP = 128
MIX_WIDTH = HG_VW + AT_W
IN_SPLITS = (HG_KW, HG_KW, HG_VW, HG_VW, AT_W, AT_W, AT_W)
IN_COLS = sum(IN_SPLITS)
D_FF = 5632
CONV_W = 3

kernel_name = "hybrid_hgrn2_chunkattn_convffn"


def rms_norm(x, gain):
    xf = x.astype(jnp.float32)
    y = xf * lax.rsqrt(jnp.mean(xf * xf, axis=-1, keepdims=True) + EPS)
    return (y * gain.astype(jnp.float32)).astype(x.dtype)


def hgrn2_chunk_scan(q, k, v, log_f):
    B, S, H, K = q.shape
    V = v.shape[-1]
    N = S // CHUNK

    def to_chunks(t):
        return t.reshape(B, N, CHUNK, H, t.shape[-1]).transpose(1, 0, 3, 2, 4)

    tri = jnp.tril(jnp.ones((CHUNK, CHUNK), dtype=bool))

    def step(state, xs):
        q_c, k_c, v_c, g_c = xs
        b = jnp.cumsum(g_c, axis=-2)
        inter = jnp.einsum('bhck,bhkv->bhcv', q_c * jnp.exp(b), state)
        diff = b[:, :, :, None, :] - b[:, :, None, :, :]
        decay = jnp.exp(jnp.where(tri[None, None, :, :, None], diff, -jnp.inf))
        attn = jnp.einsum('bhtk,bhtsk,bhsk->bhts', q_c, decay, k_c)
        intra = jnp.einsum('bhts,bhsv->bhtv', attn, v_c)
        b_last = b[:, :, -1, :]
        new_state = jnp.exp(b_last)[..., None] * state + jnp.einsum(
            'bhsk,bhsv->bhkv', k_c * jnp.exp(b_last[:, :, None, :] - b), v_c)
        return new_state, inter + intra

    state0 = jnp.zeros((B, H, K, V), jnp.float32)
    _, o = lax.scan(step, state0, (to_chunks(q), to_chunks(k), to_chunks(v), to_chunks(log_f)))
    return o.transpose(1, 0, 3, 2, 4).reshape(B, S, H, V)


def chunk_band_attention(q, k, v, rel_table):
    B, S, H, D = q.shape
    N = S // CHUNK
    pad = BAND_CHUNKS - 1
    L = BAND_CHUNKS * CHUNK
    qc = q.reshape(B, N, CHUNK, H, D)
    kp = jnp.pad(k.reshape(B, N, CHUNK, H, D), ((0, 0), (pad, 0), (0, 0), (0, 0), (0, 0)))
    vp = jnp.pad(v.reshape(B, N, CHUNK, H, D), ((0, 0), (pad, 0), (0, 0), (0, 0), (0, 0)))
    band_idx = jnp.arange(N)[:, None] + jnp.arange(BAND_CHUNKS)[None, :]
    kb = kp[:, band_idx].reshape(B, N, L, H, D)
    vb = vp[:, band_idx].reshape(B, N, L, H, D)
    scores = jnp.einsum('bnqhd,bnkhd->bhnqk', qc, kb).astype(jnp.float32) * (D ** -0.5)
    t = jnp.arange(CHUNK)
    u = jnp.arange(L)
    dist = pad * CHUNK + t[:, None] - u[None, :]
    ridx = jnp.clip(dist, -REL_CLIP, REL_CLIP) + REL_CLIP
    bias = rel_table.astype(jnp.float32)[:, ridx]
    valid = (jnp.arange(N)[:, None] - pad + jnp.arange(BAND_CHUNKS)[None, :]) >= 0
    valid = jnp.repeat(valid, CHUNK, axis=1)
    scores = jnp.where(valid[None, None, :, None, :], scores + bias[None, :, None], -jnp.inf)
    p = jax.nn.softmax(scores, axis=-1).astype(v.dtype)
    o = jnp.einsum('bhnqk,bnkhd->bnqhd', p, vb)
    return o.reshape(B, S, H * D)


def causal_dwconv(x, w, b):
    y = lax.conv_general_dilated(
        x, w[:, None, :], window_strides=(1,), padding=[(CONV_W - 1, 0)],
        dimension_numbers=('NWC', 'WIO', 'NWC'), feature_group_count=x.shape[-1])
    return y + b


def setup_inputs(seed: int = 0) -> dict:
    key = jax.random.key(seed)
    ks = jax.random.split(key, 16)
    f32 = jnp.float32
    nrm = lambda k, shape, s: jax.random.normal(k, shape, f32) * s
    return {
        "x": nrm(ks[0], (BATCH, SEQ, D_MODEL), 1.0),
        "g_mix": 1.0 + nrm(ks[1], (DEPTH, D_MODEL), 0.02),
        "w_in": nrm(ks[2], (DEPTH, D_MODEL, IN_COLS), D_MODEL ** -0.5),
        "hg_lb": nrm(ks[3], (DEPTH + 1, HG_KW), 1.0),
        "hg_out_gain": 1.0 + nrm(ks[4], (DEPTH, HG_VW), 0.02),
        "q_gain": 1.0 + nrm(ks[5], (DEPTH, AT_HDIM), 0.02),
        "k_gain": 1.0 + nrm(ks[6], (DEPTH, AT_HDIM), 0.02),
        "rel_bias": nrm(ks[7], (DEPTH, AT_HEADS, 2 * REL_CLIP + 1), 0.1),
        "at_out_gain": 1.0 + nrm(ks[8], (DEPTH, AT_W), 0.02),
        "w_out": nrm(ks[9], (DEPTH, MIX_WIDTH, D_MODEL), MIX_WIDTH ** -0.5),
        "g_ffn": 1.0 + nrm(ks[10], (DEPTH, D_MODEL), 0.02),
        "w_up": nrm(ks[11], (DEPTH, D_MODEL, 2 * D_FF), D_MODEL ** -0.5),
        "conv_w": nrm(ks[12], (DEPTH, CONV_W, 2 * D_FF), CONV_W ** -0.5),
        "conv_b": nrm(ks[13], (DEPTH, 2 * D_FF), 0.02),
        "w_down": nrm(ks[14], (DEPTH, D_FF, D_MODEL), D_FF ** -0.5),
    }


def reference(x, g_mix, w_in, hg_lb, hg_out_gain, q_gain, k_gain, rel_bias,
              at_out_gain, w_out, g_ffn, w_up, conv_w, conv_b, w_down):
    B, S, _ = x.shape
    f32 = jnp.float32
    lb_all = jnp.cumsum(jax.nn.softmax(hg_lb.astype(f32), axis=0), axis=0)
    cuts = list(np.cumsum(IN_SPLITS)[:-1])
    for l in range(DEPTH):
        h = rms_norm(x, g_mix[l])
        proj = h @ w_in[l]
        hq, hf, hi, hg, aq, ak, av = jnp.split(proj, cuts, axis=-1)

        lb = lb_all[l]
        log_f = jnp.logaddexp(jnp.log(lb), jnp.log1p(-lb) + jax.nn.log_sigmoid(hf.astype(f32)))
        k_in = -jnp.expm1(log_f)
        q_hg = jax.nn.silu(hq.astype(f32))
        shp_k = (B, S, HG_HEADS, HG_KDIM)
        o_hg = hgrn2_chunk_scan(q_hg.reshape(shp_k), k_in.reshape(shp_k),
                                hi.astype(f32).reshape(B, S, HG_HEADS, HG_VDIM),
                                log_f.reshape(shp_k))
        o_hg = o_hg * lax.rsqrt(jnp.mean(o_hg * o_hg, axis=-1, keepdims=True) + EPS)
        o_hg = (o_hg.reshape(B, S, HG_VW) * hg_out_gain[l].astype(f32)
                * jax.nn.silu(hg.astype(f32))).astype(x.dtype)

        shp_a = (B, S, AT_HEADS, AT_HDIM)
        qa = rms_norm(aq.reshape(shp_a), q_gain[l])
        ka = rms_norm(ak.reshape(shp_a), k_gain[l])
        o_at = chunk_band_attention(qa, ka, av.reshape(shp_a), rel_bias[l])
        o_at = rms_norm(o_at, at_out_gain[l])

        x = x + jnp.concatenate([o_hg, o_at], axis=-1) @ w_out[l]

        h2 = rms_norm(x, g_ffn[l])
        up = causal_dwconv(h2 @ w_up[l], conv_w[l], conv_b[l])
        a, gate = jnp.split(up, 2, axis=-1)
        x = x + (a * jax.nn.silu(gate)) @ w_down[l]
    return x
```

```python
import os
from contextlib import ExitStack
import numpy as np
import concourse.bass as bass
import concourse.mybir as mybir
from concourse.bass_utils import run_bass_kernel_spmd

F32 = mybir.dt.float32
BF16 = mybir.dt.bfloat16
AF = mybir.ActivationFunctionType
ALU = mybir.AluOpType
AX = mybir.AxisListType

NCORES = 8
D = 2048
TOK = 2048
HAL = 512
NTT = 16
NHT = 20
DFF = 5632
EPS = 1e-6
NEG = -30000.0
COMPUTE = ("pe", "act", "dve", "pool")


class StopBuild(Exception):
    pass


class Reg:
    __slots__ = ("name", "w", "rs")

    def __init__(self, name):
        self.name = name
        self.w = None
        self.rs = []


class Sched:
    def __init__(self):
        self.ops = {e: [] for e in ("pe", "act", "dve", "pool", "sync")}
        self.cnt = {e: 0 for e in COMPUTE}
        self.dcnt = {}
        self.seen = {e: {} for e in self.ops}
        self.pending = {e: [] for e in self.ops}
        self.cckeys = set()

    def _need(self, eng, ev, raw, waits):
        if ev is None:
            return
        key, val, src = ev
        if src == eng and eng in COMPUTE and key == eng:
            if not raw or eng == "pe":
                return
        if self.seen[eng].get(key, 0) >= val:
            return
        self.seen[eng][key] = val
        waits.append((key, val))

    def op(self, eng, fn, reads=(), writes=(), dsem=None, cc=False):
        waits = []
        for (k, v) in self.pending[eng]:
            if self.seen[eng].get(k, 0) < v:
                self.seen[eng][k] = v
                waits.append((k, v))
        self.pending[eng] = []
        for r in reads:
            self._need(eng, r.w, True, waits)
        for r in writes:
            self._need(eng, r.w, False, waits)
            for ev in r.rs:
                self._need(eng, ev, False, waits)
        if dsem is not None:
            inc = 1 if cc else 16
            if cc:
                self.cckeys.add(dsem)
            self.dcnt[dsem] = self.dcnt.get(dsem, 0) + inc
            ev = (dsem, self.dcnt[dsem], eng)
        else:
            self.cnt[eng] += 1
            ev = (eng, self.cnt[eng], eng)
            inc = 1
        for r in reads:
            r.rs.append(ev)
        for r in writes:
            r.w = ev
            r.rs = []
        self.ops[eng].append((waits, fn, ev[0], inc))
        return ev

    def barrier(self):
        for e in self.ops:
            for k in COMPUTE:
                if k != e and self.cnt[k] > 0:
                    self.pending[e].append((k, self.cnt[k]))
            for k, v in self.dcnt.items():
                if k not in self.cckeys:
                    self.pending[e].append((k, v))

    def replay(self, eng, e, sems):
        for (waits, fn, key, inc) in self.ops[eng]:
            for (k, v) in waits:
                e.wait_ge(sems[k], v)
            ins = fn(e)
            ins.then_inc(sems[key], inc)
        if eng == "sync":
            for k, v in self.dcnt.items():
                e.wait_ge(sems[k], v)
            for k in COMPUTE:
                if self.cnt[k] > 0:
                    e.wait_ge(sems[k], self.cnt[k])


def build_nc(debug=False):
    nc = bass.Bass("TRN2", target_bir_lowering=False)
    S = Sched()
    NOCC = bool(os.environ.get("MK_NOCC"))

    def allgather(in_ap, out_ap, reads, writes, key):
        if NOCC:
            n = in_ap.shape[0]
            return S.op("pool", lambda e: e.dma_start(out=out_ap[0:n, :], in_=in_ap), reads, writes, dsem=key)
        return S.op("pool", lambda e: e.collective_compute("AllGather", ALU.bypass, replica_groups=[list(range(NCORES))], ins=[in_ap], outs=[out_ap]), reads, writes, dsem=key, cc=True)

    def din(name, shape, dt=F32):
        return nc.dram_tensor(name, list(shape), dt, kind="ExternalInput").ap()

    xh = din("xh", [TOK + HAL, D])
    gmix_d = din("gmix_b", [128, D])
    gffn_d = din("gffn_b", [128, D])
    w_in_d = din("w_in_t", [56, 128, 2048])
    w_out_d = din("w_out_t", [128, 16 * 2048])
    w_up_sh = din("w_up_sh", [11 * 128, 2048])
    w_down_sh = din("w_down_sh", [2 * 128, 44 * 128])
    lbraw_d = din("lbraw", [128, 16])
    hgain_d = din("hgain_c", [128, 8])
    atgain_d = din("atgain_c", [128, 8])
    qkg_d = din("qkg_c", [128, 2])
    relb_d = din("relb", [8, 128, 640])
    hmask_d = din("hmask", [128, 512])
    ident_d = din("ident", [128, 128])
    tri_d = din("tri4", [128, 512])
    seg_d = din("segmask", [128, 512])
    cw_d = din("cw_c", [128, 88 * 3])
    cb_d = din("cb_c", [128, 88])
    sel_d = din("sel", [128, 32])
    alpha_d = din("alpha", [128, 16])
    y = nc.dram_tensor("y", [TOK, D], F32, kind="ExternalOutput").ap()
    ot_scr = nc.dram_tensor("ot_scr", [16, 128, TOK], BF16).ap()
    wup_in = nc.dram_tensor("wup_in", [11 * 128, 2048], BF16).ap()
    wup_all = nc.dram_tensor("wup_all", [88 * 128, 2048], BF16).ap()
    wdn_in = nc.dram_tensor("wdn_in", [2 * 128, 44 * 128], BF16).ap()
    wdn_all = nc.dram_tensor("wdn_all", [16 * 128, 44 * 128], BF16).ap()
    st_in = nc.dram_tensor("st_in", [128, 1032], F32).ap()
    st_all = nc.dram_tensor("st_all", [128 * NCORES, 1032], F32).ap()
    hh_in = nc.dram_tensor("hh_in", [2, D], BF16).ap()
    hh_all = nc.dram_tensor("hh_all", [2 * NCORES, D], BF16).ap()
    dbg = {}
    if debug:
        dbg["hT"] = nc.dram_tensor("dbg_hT", [128, 16 * 2560], BF16, kind="ExternalOutput").ap()
        dbg["st"] = nc.dram_tensor("dbg_st", [128, 1032], F32, kind="ExternalOutput").ap()
        dbg["tinit"] = nc.dram_tensor("dbg_tinit", [128, 1024], F32, kind="ExternalOutput").ap()
        dbg["ot"] = nc.dram_tensor("dbg_ot", [16, 128, TOK], BF16, kind="ExternalOutput").ap()
        dbg["ssat"] = nc.dram_tensor("dbg_ssat", [128, 16], F32, kind="ExternalOutput").ap()
        dbg["h2T"] = nc.dram_tensor("dbg_h2T", [128, 16 * 2050], BF16, kind="ExternalOutput").ap()

    es = ExitStack()
    with es:
        AW = 52000
        arena = es.enter_context(nc.sbuf_tensor("arena", [128, AW], F32))
        psF = [es.enter_context(nc.psum_tensor(f"psF{i}", [128, 512], F32)) for i in range(6)]
        psB = [es.enter_context(nc.psum_tensor(f"psB{i}", [128, 1024], BF16)) for i in range(2)]
        psF = [p[:, :] for p in psF]
        psB = [p[:, :] for p in psB]
        rF = [Reg(f"psF{i}") for i in range(6)]
        rB = [Reg(f"psB{i}") for i in range(2)]

        class Arena:
            def __init__(self):
                self.off = 0

            def f32(self, n):
                a = arena[:, self.off:self.off + n]
                self.off += n
                assert self.off <= AW, self.off
                return a

            def bf(self, n):
                assert n % 2 == 0
                return self.f32(n // 2).bitcast(BF16)

        A = Arena()
        rid = [0]

        def R(name="r"):
            rid[0] += 1
            return Reg(f"{name}{rid[0]}")

        ident_f = A.f32(128); r_identf = R()
        ident_b = A.bf(128); r_identb = R()
        lbraw = A.f32(16)
        lb = A.f32(8); om = A.f32(8); nom = A.f32(8); r_lb = R()
        hgain = A.f32(8); atgain = A.f32(8); qkg = A.f32(2); r_small = R()
        qsc = A.f32(1)
        alpha = A.f32(16)
        oma = A.f32(8)
        cw = A.f32(88 * 3); cb = A.f32(88)
        ssat = A.f32(16); r_ssat = R()
        rat = A.f32(16)
        ones_b = A.bf(128)
        hmask_f = A.f32(512)
        sel_b = A.bf(32)
        tri_b = A.bf(512)
        segm = A.f32(512)
        r_const = R()

        def dma_sync(out, in_, reads, writes, key):
            return S.op("sync", lambda e: e.dma_start(out=out, in_=in_), reads, writes, dsem=key)

        def dma_pool(out, in_, reads, writes, key):
            return S.op("pool", lambda e: e.dma_start(out=out, in_=in_), reads, writes, dsem=key)

        dma_sync(ident_f, ident_d, [], [r_identf], "c0")
        dma_pool(ident_b, ident_d, [], [r_identb], "c1")
        dma_sync(lbraw, lbraw_d, [], [r_lb], "c2")
        dma_sync(hgain, hgain_d, [], [r_small], "c3")
        dma_sync(atgain, atgain_d, [], [r_small], "c3")
        dma_sync(qkg, qkg_d, [], [r_small], "c3")
        dma_sync(alpha, alpha_d, [], [r_small], "c3")
        dma_sync(cw, cw_d, [], [r_small], "c3")
        dma_sync(cb, cb_d, [], [r_small], "c3")
        dma_sync(segm, seg_d, [], [r_const], "c4")
        dma_pool(tri_b, tri_d, [], [r_const], "c5")
        dma_sync(hmask_f, hmask_d, [], [r_const], "c4")
        dma_pool(sel_b, sel_d, [], [r_const], "c5")
        S.op("dve", lambda e: e.memset(ones_b, 1.0), [], [r_const])
        S.op("dve", lambda e: e.memset(ssat, 0.0), [], [r_ssat])
        S.op("dve", lambda e: e.tensor_tensor(out=lb, in0=lbraw[:, 0:8], in1=lbraw[:, 8:16], op=ALU.subtract), [r_lb], [r_lb])
        S.op("act", lambda e: e.activation(out=lb, in_=lb, func=AF.Sigmoid), [r_lb], [r_lb])
        S.op("dve", lambda e: e.tensor_scalar(out=om, in0=lb, scalar1=-1.0, scalar2=1.0, op0=ALU.mult, op1=ALU.add), [r_lb], [r_lb])
        S.op("dve", lambda e: e.tensor_scalar(out=nom, in0=om, scalar1=-1.0, scalar2=None, op0=ALU.mult), [r_lb], [r_lb])
        S.op("dve", lambda e: e.tensor_scalar(out=qsc, in0=qkg[:, 0:1], scalar1=float(128 ** -0.5), scalar2=None, op0=ALU.mult), [r_small], [r_small])
        S.op("dve", lambda e: e.tensor_scalar(out=oma, in0=alpha[:, 0:8], scalar1=-1.0, scalar2=1.0, op0=ALU.mult, op1=ALU.add), [r_small], [r_small])

        r_wupin = R(); r_wupall = R(); r_wdnin = R(); r_wdnall = R()
        dma_pool(wup_in, w_up_sh, [], [r_wupin], "wg0")
        for q in range(4):
            dma_pool(wdn_in[:, q * 1408:(q + 1) * 1408], w_down_sh[:, q * 1408:(q + 1) * 1408], [], [r_wdnin], "wg1")
        allgather(wup_in, wup_all, [r_wupin], [r_wupall], "wg2")
        allgather(wdn_in, wdn_all, [r_wdnin], [r_wdnall], "wg3")
        persist_off = A.off
        hT = A.bf(16 * 2560)
        hT3 = hT.rearrange("p (c t) -> p c t", t=2560)
        r_hT = [R("hT") for _ in range(NHT)]

        phase_off = A.off

        def rms_rstd(ss_ap, n, dim, r_ss, tmp):
            ms = tmp[:, 0:n]
            rs = tmp[:, n:2 * n]
            S.op("dve", lambda e: e.tensor_scalar(out=ms, in0=ss_ap, scalar1=1.0 / dim, scalar2=EPS, op0=ALU.mult, op1=ALU.add), [r_ss], [r_ss])
            S.op("act", lambda e: e.activation(out=ms, in_=ms, func=AF.Sqrt), [r_ss], [r_ss])
            S.op("dve", lambda e: e.reciprocal(out=rs, in_=ms), [r_ss], [r_ss])
            return rs

        def transposes(src_aps, bank, rbank, reads, ident=None):
            idn = ident_b

            def fn(e):
                ins = None
                for j, s in enumerate(src_aps):
                    ins = e.transpose(out=psB[bank][:, j * 128:(j + 1) * 128], in_=s, identity=idn)
                return ins
            S.op("pe", fn, list(reads) + [r_identb], [rbank])

        STOP = os.environ.get("MK_STOP", "")

        def check_stop(tag):
            if STOP == tag:
                raise StopBuild()

        try:
            xbuf = [A.f32(D) for _ in range(2)]; r_x = [R("x") for _ in range(2)]
            hb = [A.bf(D) for _ in range(2)]; r_hb = [R("hb") for _ in range(2)]
            junk = A.bf(D); r_junk = R()
            gmix = A.f32(D); r_gmix = R()
            stA = A.f32(4 * NHT); r_stA = [R() for _ in range(NHT)]
            dma_sync(gmix, gmix_d, [], [r_gmix], "c6")

            def norm_tile(xt, r_xt, gains, r_g, hbt, r_hbt, st, r_st, junk):
                S.op("act", lambda e: e.activation(out=junk, in_=xt, func=AF.Square, accum_out=st[:, 0:1]), [r_xt], [r_junk, r_st])
                rs = rms_rstd(st[:, 0:1], 1, D, r_st, st[:, 1:3])
                S.op("dve", lambda e: e.scalar_tensor_tensor(out=hbt, in0=xt, scalar=rs, in1=gains, op0=ALU.mult, op1=ALU.mult), [r_xt, r_st, r_g], [r_hbt])

            def transpose_to(hbt, r_hbt, dst3, col0, r_dst):
                for half in range(2):
                    transposes([hbt[:, (half * 8 + j) * 128:(half * 8 + j + 1) * 128] for j in range(8)], half, rB[half], [r_hbt])
                    src = psB[half].rearrange("p (c t) -> p c t", t=128)
                    dst = dst3[:, half * 8:(half + 1) * 8, col0:col0 + 128]
                    if half == 0:
                        S.op("act", lambda e, s=src, d=dst: e.activation(out=d, in_=s, func=AF.Copy), [rB[half]], [r_dst])
                    else:
                        S.op("dve", lambda e, s=src, d=dst: e.tensor_copy(out=d, in_=s), [rB[half]], [r_dst])

            for tt in range(NHT):
                b = tt % 2
                dma_sync(xbuf[b], xh[tt * 128:(tt + 1) * 128, :], [], [r_x[b]], f"x{b}")
                norm_tile(xbuf[b], r_x[b], gmix, r_gmix, hb[b], r_hb[b], stA[:, 4 * tt:4 * tt + 4], r_stA[tt], junk)
                transpose_to(hb[b], r_hb[b], hT3, tt * 128, r_hT[tt])
            if debug:
                dma_sync(dbg["hT"], hT, r_hT, [], "dbg")

            check_stop("A")
            S.barrier()
            A.off = phase_off

            NW = 6
            wbuf = [A.bf(2048) for _ in range(NW)]
            r_w = [R("w") for _ in range(NW)]
            wi = [0]

            def load_w(src):
                s = wi[0] % NW
                wi[0] += 1
                dma_pool(wbuf[s], src, [], [r_w[s]], f"w{s}")
                return wbuf[s].rearrange("p (c n) -> p c n", n=128), r_w[s]

            def hblk_regs(blk):
                return [r_hT[4 + blk * 4 + j] for j in range(4)]

            def proj_fm(w3, r_wt, col0, ncols, bank, hregs):
                def fn(e):
                    ins = None
                    for c in range(16):
                        ins = e.matmul(psF[bank][:, 0:ncols], lhsT=w3[:, c, :], rhs=hT3[:, c, col0:col0 + ncols], start=(c == 0), stop=(c == 15))
                    return ins
                S.op("pe", fn, [r_wt] + hregs, [rF[bank]])

            def proj_tm(w3, r_wt, col0, ntile, bank, hregs):
                def fn(e):
                    ins = None
                    for j in range(ntile):
                        for c in range(16):
                            ins = e.matmul(psF[bank][:, j * 128:(j + 1) * 128], lhsT=hT3[:, c, col0 + j * 128:col0 + (j + 1) * 128], rhs=w3[:, c, :], start=(c == 0), stop=(c == 15))
                    return ins
                S.op("pe", fn, [r_wt] + hregs, [rF[bank]])

            def T32():
                return A.f32(512), R("t")

            sig, r_sig = T32(); logf, r_logf = T32(); kk, r_kk = T32(); bb, r_bb = T32()
            d1, r_d1 = T32(); qq, r_qq = T32(); e0, r_e0 = T32(); d2, r_d2 = T32(); e2, r_e2 = T32(); e3, r_e3 = T32()
            sg, r_sg = T32()
            khat = A.bf(512); r_khat = R(); qt = A.bf(512); r_qt = R(); kt = A.bf(512); r_kt = R()
            qha = A.bf(512); qhb = A.bf(512); r_qhat = R()
            atm = A.bf(512); r_atm = R(); og = A.bf(512); r_og = R()
            otile = [A.bf(512) for _ in range(2)]; r_otile = [R() for _ in range(2)]
            v_tm = A.bf(512); r_vtm = R(); khat_lo = A.bf(512); khat_hi = A.bf(512); r_khtm = R()
            Sst = A.f32(128); r_S = R()
            Sbf = A.bf(8 * 128); r_Sbf = R()
            stpack = A.f32(1032); r_stp = R()
            Tinit = A.f32(1024); r_T = R()
            bCall = A.f32(8 * 32); r_bC = R()
            dec = A.f32(8); r_dec = R()
            sto = A.f32(12); r_sto = R()
            b3 = bb.rearrange("p (c t) -> p c t", t=64)

            def v3(ap):
                return ap.rearrange("p (c t) -> p c t", t=64)

            def v4(ap):
                return ap.rearrange("p (j u t) -> p j u t", u=2, t=64)

            S.op("dve", lambda e: e.memset(khat_lo, 0.0), [], [r_khtm])
            S.op("dve", lambda e: e.memset(khat_hi, 0.0), [], [r_khtm])
            S.op("dve", lambda e: e.memset(qha, 0.0), [], [r_qhat])
            S.op("dve", lambda e: e.memset(qhb, 0.0), [], [r_qhat])

            def hgrn_head(h, full):
                wf, r_wf = load_w(w_in_d[8 + h])
                wv, r_wv = load_w(w_in_d[16 + h])
                if full:
                    wq, r_wq = load_w(w_in_d[h])
                    wg, r_wg = load_w(w_in_d[24 + h])
                    S.op("dve", lambda e: e.tensor_copy(out=Sst, in_=Tinit[:, h * 128:(h + 1) * 128]), [r_T], [r_S])
                else:
                    S.op("dve", lambda e: e.memset(Sst, 0.0), [], [r_S])
                for blk in range(4):
                    col0 = HAL + blk * 512
                    hregs = hblk_regs(blk)
                    proj_fm(wf, r_wf, col0, 512, 0, hregs)
                    S.op("act", lambda e: e.activation(out=sig, in_=psF[0], func=AF.Sigmoid), [rF[0]], [r_sig])
                    S.op("act", lambda e: e.activation(out=logf, in_=sig, func=AF.Ln, bias=lb[:, h:h + 1], scale=om[:, h:h + 1]), [r_sig, r_lb], [r_logf])
                    S.op("dve", lambda e: e.tensor_scalar(out=kk, in0=sig, scalar1=nom[:, h:h + 1], scalar2=om[:, h:h + 1], op0=ALU.mult, op1=ALU.add), [r_sig, r_lb], [r_kk])
                    S.op("dve", lambda e: e.tensor_tensor_scan(out=bb, data0=segm, data1=logf, initial=0.0, op0=ALU.mult, op1=ALU.add), [r_logf, r_const], [r_bb])
                    proj_tm(wv, r_wv, col0, 4, 1, hregs)
                    S.op("act", lambda e: e.activation(out=v_tm, in_=psF[1], func=AF.Copy), [rF[1]], [r_vtm])
                    bC = b3[:, :, 63:64]
                    S.op("dve", lambda e: e.scalar_tensor_tensor(out=v3(d1), in0=b3, scalar=-1.0, in1=bC.broadcast_to([128, 8, 64]), op0=ALU.mult, op1=ALU.add), [r_bb], [r_d1])
                    S.op("act", lambda e: e.activation(out=d1, in_=d1, func=AF.Exp), [r_d1], [r_d1])
                    S.op("dve", lambda e: e.tensor_tensor(out=khat, in0=kk, in1=d1, op=ALU.mult), [r_kk, r_d1], [r_khat])
                    transposes([khat[:, j * 128:(j + 1) * 128] for j in range(4)], 0, rB[0], [r_khat])
                    S.op("dve", lambda e: e.tensor_copy(out=khat_lo[0:64, :], in_=psB[0][0:64, 0:512]), [rB[0]], [r_khtm])
                    S.op("dve", lambda e: e.tensor_copy(out=khat_hi[64:128, :], in_=psB[0][64:128, 0:512]), [rB[0]], [r_khtm])
                    bcv = bCall[:, h * 32 + blk * 8:h * 32 + blk * 8 + 8]
                    S.op("dve", lambda e, bcv=bcv: e.tensor_copy(out=bcv.rearrange("p (c o) -> p c o", o=1), in_=bC), [r_bb], [r_bC])
                    S.op("act", lambda e, bcv=bcv: e.activation(out=dec, in_=bcv, func=AF.Exp), [r_bC], [r_dec])
                    if full:
                        proj_fm(wq, r_wq, col0, 512, 2, hregs)
                        S.op("act", lambda e: e.activation(out=qq, in_=psF[2], func=AF.Silu), [rF[2]], [r_qq])
                        S.op("act", lambda e: e.activation(out=e0, in_=bb, func=AF.Exp), [r_bb], [r_e0])
                        S.op("dve", lambda e: e.tensor_tensor(out=v4(qha)[:, :, 0, :], in0=v4(qq)[:, :, 0, :], in1=v4(e0)[:, :, 0, :], op=ALU.mult), [r_qq, r_e0], [r_qhat])
                        S.op("dve", lambda e: e.tensor_tensor(out=v4(qhb)[:, :, 1, :], in0=v4(qq)[:, :, 1, :], in1=v4(e0)[:, :, 1, :], op=ALU.mult), [r_qq, r_e0], [r_qhat])
                        if h == 0 and blk == 0:
                            check_stop("P2a")
                        bm = b3[:, :, 31:32]
                        S.op("dve", lambda e: e.tensor_tensor(out=v3(d2), in0=b3, in1=bm.broadcast_to([128, 8, 64]), op=ALU.subtract), [r_bb], [r_d2])
                        S.op("act", lambda e: e.activation(out=e2, in_=d2, func=AF.Exp), [r_d2], [r_e2])
                        S.op("act", lambda e: e.activation(out=e3, in_=d2, func=AF.Exp, scale=-1.0), [r_d2], [r_e3])
                        S.op("dve", lambda e: e.tensor_tensor(out=qt, in0=qq, in1=e2, op=ALU.mult), [r_qq, r_e2], [r_qt])
                        S.op("dve", lambda e: e.tensor_tensor(out=kt, in0=kk, in1=e3, op=ALU.mult), [r_kk, r_e3], [r_kt])
                        proj_tm(wg, r_wg, col0, 4, 3, hregs)
                        S.op("act", lambda e: e.activation(out=sg, in_=psF[3], func=AF.Silu), [rF[3]], [r_sg])

                        def fn_at(e):
                            ins = None
                            for j in range(4):
                                cs = slice(j * 128, (j + 1) * 128)
                                ins = e.matmul(psF[0][:, cs], lhsT=kt[:, cs], rhs=qt[:, cs], start=True, stop=True)
                            return ins
                        S.op("pe", fn_at, [r_kt, r_qt], [rF[0]])
                        S.op("dve", lambda e: e.tensor_tensor(out=atm, in0=psF[0], in1=tri_b, op=ALU.mult), [rF[0], r_const], [r_atm])
                        if h == 0 and blk == 0:
                            check_stop("P2b")

                    def fn_u(e):
                        ins = None
                        for c in range(8):
                            j, u = c // 2, c % 2
                            cs = slice(j * 128, (j + 1) * 128)
                            bank = 4 + u
                            kh = khat_lo if u == 0 else khat_hi
                            ins = e.matmul(psF[bank][:, j * 128:(j + 1) * 128], lhsT=kh[:, cs], rhs=v_tm[:, cs], start=True, stop=True)
                        return ins
                    if not full and h == 0 and blk == 0:
                        check_stop("B1a")
                    S.op("pe", fn_u, [r_khtm, r_vtm], [rF[4], rF[5]])
                    if not full and h == 0 and blk == 0:
                        check_stop("B1b")
                    for c in range(8):
                        bank = 4 + c % 2
                        ucs = slice((c // 2) * 128, (c // 2 + 1) * 128)
                        if full:
                            S.op("act", lambda e, c=c: e.activation(out=Sbf[:, c * 128:(c + 1) * 128], in_=Sst, func=AF.Copy), [r_S], [r_Sbf])
                        S.op("dve", lambda e, c=c, bank=bank, ucs=ucs: e.scalar_tensor_tensor(out=Sst, in0=Sst, scalar=dec[:, c:c + 1], in1=psF[bank][:, ucs], op0=ALU.mult, op1=ALU.add), [r_S, r_dec, rF[bank]], [r_S])
                    if full:
                        if h == 0 and blk == 0:
                            check_stop("P2c")

                        def fn_o(e):
                            ins = None
                            for j in range(4):
                                cs = slice(j * 128, (j + 1) * 128)
                                e.matmul(psF[2][:, cs], lhsT=qha[:, cs], rhs=Sbf[:, (2 * j) * 128:(2 * j + 1) * 128], start=True, stop=False)
                                e.matmul(psF[2][:, cs], lhsT=qhb[:, cs], rhs=Sbf[:, (2 * j + 1) * 128:(2 * j + 2) * 128], start=False, stop=False)
                                ins = e.matmul(psF[2][:, cs], lhsT=atm[:, cs], rhs=v_tm[:, cs], start=False, stop=True)
                            return ins
                        S.op("pe", fn_o, [r_qhat, r_Sbf, r_atm, r_vtm], [rF[2]])
                        if h == 0 and blk == 0:
                            check_stop("P2d")
                        for j in range(4):
                            cs = slice(j * 128, (j + 1) * 128)
                            S.op("act", lambda e, cs=cs, j=j: e.activation(out=junkB[:, cs], in_=psF[2][:, cs], func=AF.Square, accum_out=sto[:, j:j + 1]), [rF[2]], [r_junk, r_sto])
                        rs = rms_rstd(sto[:, 0:4], 4, 128, r_sto, sto[:, 4:12])
                        for j in range(4):
                            cs = slice(j * 128, (j + 1) * 128)
                            S.op("dve", lambda e, cs=cs, j=j: e.scalar_tensor_tensor(out=og[:, cs], in0=psF[2][:, cs], scalar=rs[:, j:j + 1], in1=sg[:, cs], op0=ALU.mult, op1=ALU.mult), [rF[2], r_sto, r_sg], [r_og])
                        transposes([og[:, j * 128:(j + 1) * 128] for j in range(4)], 1, rB[1], [r_og])
                        ob = (h * 4 + blk) % 2
                        S.op("dve", lambda e, ob=ob: e.tensor_scalar(out=otile[ob], in0=psB[1][:, 0:512], scalar1=hgain[:, h:h + 1], scalar2=None, op0=ALU.mult), [rB[1], r_small], [r_otile[ob]])
                        dma_sync(ot_scr[h, :, blk * 512:(blk + 1) * 512], otile[ob], [r_otile[ob]], [r_otscr[h]], f"ot{ob}")
                if not full:
                    S.op("dve", lambda e: e.tensor_copy(out=stpack[:, h * 128:(h + 1) * 128], in_=Sst), [r_S], [r_stp])
                    if h == 0:
                        check_stop("B1H0")

            r_otscr = [R("otscr") for _ in range(16)]

            junkB = A.bf(D)
            for h in range(8):
                hgrn_head(h, False)
            S.op("dve", lambda e: e.tensor_reduce(out=stpack[:, 1024:1032], in_=bCall.rearrange("p (h c) -> p h c", c=32), axis=AX.X, op=ALU.add), [r_bC], [r_stp])
            S.op("act", lambda e: e.activation(out=stpack[:, 1024:1032], in_=stpack[:, 1024:1032], func=AF.Exp), [r_stp], [r_stp])
            r_stin = R(); r_stall = R()
            dma_pool(st_in, stpack, [r_stp], [r_stin], "ag1a")
            allgather(st_in, st_all, [r_stin], [r_stall], "ag1b")
            if debug:
                dma_sync(dbg["st"], stpack, [r_stp], [], "dbg")

            check_stop("B1")
            knT = A.bf(2560); r_knT = R(); qnT = A.bf(2048); r_qnT = R()
            va = A.bf(NHT * 128); r_va = R()
            relb = [A.f32(640) for _ in range(2)]; r_relb = [R() for _ in range(2)]
            sc = A.f32(640); r_sc = R(); sq = A.f32(512); r_sq = R(); kn = A.bf(512); r_kn = R()
            Pm = A.bf(640); r_P = R(); PT = A.bf(640); r_PT = R()
            ob16 = A.bf(128); r_ob = R()
            oTat = [A.bf(2048) for _ in range(1)]; r_oTat = R()
            sta = A.f32(16); r_sta = R()
            rsall = A.f32(16); rinvall = A.f32(16); r_rsum = R()
            ssraw = A.f32(16); r_ssraw = R()
            ssh = A.f32(16); r_ssh = R()

            def qk_side(w3, r_wt, col0, ngroups, dstT, r_dst, gain_col):
                for g in range(ngroups):
                    c0 = col0 + g * 512
                    hregs = [r_hT[(c0 // 128) + j] for j in range(4)]
                    proj_tm(w3, r_wt, c0, 4, 0, hregs)
                    S.op("act", lambda e: e.activation(out=sq, in_=psF[0], func=AF.Square), [rF[0]], [r_sq])
                    S.op("dve", lambda e: e.tensor_reduce(out=sta[:, 0:4], in_=sq.rearrange("p (c t) -> p c t", t=128), axis=AX.X, op=ALU.add), [r_sq], [r_sta])
                    rs = rms_rstd(sta[:, 0:4], 4, 128, r_sta, sta[:, 4:12])
                    S.op("dve", lambda e, rs=rs: e.tensor_tensor(out=kn.rearrange("p (c t) -> p c t", t=128), in0=psF[0].rearrange("p (c t) -> p c t", t=128), in1=rs.rearrange("p (c o) -> p c o", o=1).broadcast_to([128, 4, 128]), op=ALU.mult), [rF[0], r_sta], [r_kn])
                    transposes([kn[:, j * 128:(j + 1) * 128] for j in range(4)], 0, rB[0], [r_kn])
                    dcol = c0 - col0
                    S.op("dve", lambda e, dcol=dcol: e.tensor_scalar(out=dstT[:, dcol:dcol + 512], in0=psB[0][:, 0:512], scalar1=gain_col, scalar2=None, op0=ALU.mult), [rB[0], r_small], [r_dst])

            def attn_head(h):
                wq, r_wq = load_w(w_in_d[32 + h])
                wk, r_wk = load_w(w_in_d[40 + h])
                wv, r_wv = load_w(w_in_d[48 + h])
                rb = relb[h % 2]; r_rb = r_relb[h % 2]
                dma_sync(rb, relb_d[h], [], [r_rb], f"rb{h % 2}")
                qk_side(wk, r_wk, 0, 5, knT, r_knT, qkg[:, 1:2])
                for g in range(5):
                    hregs = [r_hT[g * 4 + j] for j in range(4)]
                    proj_tm(wv, r_wv, g * 512, 4, 1, hregs)
                    S.op("act", lambda e, g=g: e.activation(out=va[:, g * 512:(g + 1) * 512], in_=psF[1], func=AF.Copy), [rF[1]], [r_va])
                qk_side(wq, r_wq, HAL, 4, qnT, r_qnT, qsc[:, 0:1])
                if h == 0:
                    check_stop("ATTQ")
                for i in range(NTT):
                    k0 = i * 128

                    def fn_s(e, i=i, k0=k0):
                        nm = 512 - 128 * i
                        e.matmul(psF[2][:, 0:512], lhsT=qnT[:, k0:k0 + 128], rhs=knT[:, k0:k0 + 512], start=True, stop=True)
                        return e.matmul(psF[3][:, 0:128], lhsT=qnT[:, k0:k0 + 128], rhs=knT[:, k0 + 512:k0 + 640], start=True, stop=True)
                    S.op("pe", fn_s, [r_qnT, r_knT, r_const], [rF[2], rF[3]])
                    S.op("dve", lambda e: e.tensor_tensor(out=sc[:, 0:512], in0=psF[2], in1=rb[:, 0:512], op=ALU.add), [rF[2], r_rb], [r_sc])
                    S.op("dve", lambda e: e.tensor_tensor(out=sc[:, 512:640], in0=psF[3][:, 0:128], in1=rb[:, 512:640], op=ALU.add), [rF[3], r_rb], [r_sc])
                    if i < 4:
                        S.op("dve", lambda e, k0=k0: e.tensor_tensor(out=sc[:, 0:512 - k0], in0=sc[:, 0:512 - k0], in1=hmask_f[:, k0:512], op=ALU.add), [r_sc, r_const], [r_sc])
                    if h == 0 and i == 0:
                        check_stop("ATTS")
                    S.op("act", lambda e, i=i: e.activation(out=Pm, in_=sc, func=AF.Exp, accum_out=rsall[:, i:i + 1]), [r_sc], [r_P, r_rsum])
                    transposes([Pm[:, j * 128:(j + 1) * 128] for j in range(5)], 1, rB[1], [r_P])
                    S.op("act", lambda e: e.activation(out=PT, in_=psB[1][:, 0:640], func=AF.Copy), [rB[1]], [r_PT])

                    def fn_pv(e, i=i):
                        ins = None
                        for j in range(5):
                            ins = e.matmul(psF[5][:, 0:128], lhsT=PT[:, j * 128:(j + 1) * 128], rhs=va[:, (i + j) * 128:(i + j + 1) * 128], start=(j == 0), stop=(j == 4))
                        return ins
                    S.op("pe", fn_pv, [r_PT, r_va], [rF[5]])
                    if h == 0 and i == 0:
                        check_stop("ATTP")
                    S.op("dve", lambda e, i=i: e.reciprocal(out=rinvall[:, i:i + 1], in_=rsall[:, i:i + 1]), [r_rsum], [r_rsum])
                    S.op("dve", lambda e, i=i: e.tensor_scalar(out=ob16, in0=psF[5][:, 0:128], scalar1=rinvall[:, i:i + 1], scalar2=None, op0=ALU.mult), [rF[5], r_rsum], [r_ob])
                    S.op("act", lambda e, i=i: e.activation(out=junkB[:, 0:128], in_=psF[5][:, 0:128], func=AF.Square, accum_out=ssraw[:, i:i + 1]), [rF[5]], [r_junk, r_ssraw])
                    transposes([ob16], 0, rB[0], [r_ob])
                    S.op("dve", lambda e, k0=k0: e.tensor_scalar(out=oTat[0][:, k0:k0 + 128], in0=psB[0][:, 0:128], scalar1=atgain[:, h:h + 1], scalar2=None, op0=ALU.mult), [rB[0], r_small], [r_oTat])
                    if h == 0 and i == 0:
                        check_stop("ATTE")
                    if h == 0 and i == 3:
                        check_stop("ATT3")
                    if h == 0 and i == 8:
                        check_stop("ATT8")
                if h == 0:
                    check_stop("ATTH")
                dma_sync(ot_scr[8 + h], oTat[0], [r_oTat], [r_otscr[8 + h]], "otat")
                S.op("dve", lambda e: e.tensor_tensor(out=ssh, in0=ssraw, in1=rinvall, op=ALU.mult), [r_ssraw, r_rsum], [r_ssh])
                S.op("dve", lambda e: e.tensor_tensor(out=ssh, in0=ssh, in1=rinvall, op=ALU.mult), [r_ssh, r_rsum], [r_ssh])
                S.op("dve", lambda e: e.tensor_tensor(out=ssat, in0=ssat, in1=ssh, op=ALU.add), [r_ssat, r_ssh], [r_ssat])

            for h in range(8):
                attn_head(h)
            if debug:
                dma_sync(dbg["ssat"], ssat, [r_ssat], [], "dbg")

            check_stop("ATT")
            slab = [A.f32(1032) for _ in range(2)]; r_slab = [R() for _ in range(2)]
            cj = A.f32(8); r_cj = R()
            S.op("dve", lambda e: e.memset(Tinit, 0.0), [], [r_T])
            for j in range(NCORES - 1):
                sl = slab[j % 2]; r_sl = r_slab[j % 2]
                dma_sync(sl, st_all[j * 128:(j + 1) * 128, :], [r_stall], [r_sl], f"slab{j % 2}")
                S.op("dve", lambda e, sl=sl, j=j: e.tensor_scalar(out=cj, in0=sl[:, 1024:1032], scalar1=alpha[:, j:j + 1], scalar2=oma[:, j:j + 1], op0=ALU.mult, op1=ALU.add), [r_sl, r_small], [r_cj])
                S.op("dve", lambda e, sl=sl, j=j: e.tensor_scalar(out=sl[:, 0:1024], in0=sl[:, 0:1024], scalar1=alpha[:, j:j + 1], scalar2=None, op0=ALU.mult), [r_sl, r_small], [r_sl])
                for h in range(8):
                    hs = slice(h * 128, (h + 1) * 128)
                    S.op("dve", lambda e, sl=sl, hs=hs, h=h: e.scalar_tensor_tensor(out=Tinit[:, hs], in0=Tinit[:, hs], scalar=cj[:, h:h + 1], in1=sl[:, hs], op0=ALU.mult, op1=ALU.add), [r_T, r_cj, r_sl], [r_T])
            if debug:
                dma_sync(dbg["tinit"], Tinit, [r_T], [], "dbg")

            check_stop("CMB")
            for h in range(8):
                hgrn_head(h, True)
            if debug:
                for c in range(16):
                    S.op("sync", lambda e, c=c: e.dma_start(out=dbg["ot"][c], in_=ot_scr[c]), [r_otscr[c]], [], dsem="dbg")

            check_stop("B3")
            S.barrier()
            A.off = persist_off

            h2T = A.bf(16 * 2050)
            h2T3 = h2T.rearrange("p (c t) -> p c t", t=2050)
            r_h2T = [R("h2T") for _ in range(NTT)]
            r_h2halo = R()
            phase_off = A.off
            wout = A.bf(16 * 2048); r_wout = R()
            wout3 = wout.rearrange("p (c n) -> p c n", n=2048)
            for c in range(16):
                dma_pool(wout[:, c * 2048:(c + 1) * 2048], w_out_d[:, c * 2048:(c + 1) * 2048], [], [r_wout], "wout")
            gffn = A.f32(D); r_gffn = R()
            dma_sync(gffn, gffn_d, [], [r_gffn], "c6")
            otl = [A.bf(2048) for _ in range(2)]; r_otl = [R() for _ in range(2)]
            xc = [A.f32(D) for _ in range(2)]; r_xc = [R() for _ in range(2)]
            x2 = [A.f32(D) for _ in range(2)]; r_x2 = [R() for _ in range(2)]
            h2b = [A.bf(D) for _ in range(2)]; r_h2b = [R() for _ in range(2)]
            tmpc = A.f32(512); r_tmpc = R()
            junkC = A.bf(D)
            stC = A.f32(4 * NTT); r_stC = [R() for _ in range(NTT)]
            r_y = [R("y") for _ in range(NTT)]
            ratt = A.f32(32)
            rat_ap = rms_rstd(ssat, 16, 1024, r_ssat, ratt)

            check_stop("C0")
            for tt in range(NTT):
                b = tt % 2
                o3 = otl[b].rearrange("p (c t) -> p c t", t=128)
                dma_sync(o3, ot_scr[:, :, tt * 128:(tt + 1) * 128].rearrange("c p t -> p c t"), r_otscr, [r_otl[b]], f"otl{b}")
                dma_sync(xc[b], xh[HAL + tt * 128:HAL + (tt + 1) * 128, :], [], [r_xc[b]], f"xc{b}")
                for ct in range(4):
                    cs = slice(ct * 512, (ct + 1) * 512)
                    pa = (ct % 2) * 2

                    def fn_c(e, o3=o3, cs=cs, pa=pa):
                        ins = None
                        for c in range(8):
                            ins = e.matmul(psF[pa], lhsT=o3[:, c, :], rhs=wout3[:, c, cs], start=(c == 0), stop=(c == 7))
                        for c in range(8, 16):
                            ins = e.matmul(psF[pa + 1], lhsT=o3[:, c, :], rhs=wout3[:, c, cs], start=(c == 8), stop=(c == 15))
                        return ins
                    S.op("pe", fn_c, [r_otl[b], r_wout], [rF[pa], rF[pa + 1]])
                    S.op("dve", lambda e, b=b, cs=cs, pa=pa, tt=tt: e.scalar_tensor_tensor(out=tmpc, in0=psF[pa + 1], scalar=rat_ap[:, tt:tt + 1], in1=xc[b][:, cs], op0=ALU.mult, op1=ALU.add), [rF[pa + 1], r_ssat, r_xc[b]], [r_tmpc])
                    S.op("dve", lambda e, b=b, cs=cs, pa=pa: e.tensor_tensor(out=x2[b][:, cs], in0=psF[pa], in1=tmpc, op=ALU.add), [rF[pa], r_tmpc], [r_x2[b]])
                if tt == 0:
                    check_stop("C1")
                dma_sync(y[tt * 128:(tt + 1) * 128, :], x2[b], [r_x2[b]], [r_y[tt]], f"ys{b}")
                norm_tile(x2[b], r_x2[b], gffn, r_gffn, h2b[b], r_h2b[b], stC[:, 4 * tt:4 * tt + 4], r_stC[tt], junkC)
                transpose_to(h2b[b], r_h2b[b], h2T3, 2 + tt * 128, r_h2T[tt])
                if tt == 0:
                    check_stop("C2")
                if tt == NTT - 1:
                    check_stop("C3")
                    r_hhin = R(); r_hhall = R()
                    dma_pool(hh_in, h2b[b][126:128, :], [r_h2b[b]], [r_hhin], "ag2a")
                    allgather(hh_in, hh_all, [r_hhin], [r_hhall], "ag2b")
            hall = A.bf(D); r_hall = R()
            S.op("dve", lambda e: e.memset(hall, 0.0), [], [r_hall])
            dma_sync(hall[0:16, :], hh_all, [r_hhall], [r_hall], "hall")

            def fn_sel(e):
                ins = None
                for c in range(16):
                    ins = e.matmul(psF[4][:, 32 * c:32 * c + 32], lhsT=hall[:, c * 128:(c + 1) * 128], rhs=sel_b[:, 0:32], start=True, stop=True)
                return ins
            check_stop("C4")
            S.op("pe", fn_sel, [r_hall, r_const], [rF[4]])
            S.op("dve", lambda e: e.tensor_copy(out=h2T3[:, :, 0:2], in_=psF[4].rearrange("p (c t) -> p c t", t=32)[:, :, 0:2]), [rF[4]], [r_h2halo])
            if debug:
                dma_sync(dbg["h2T"], h2T, r_h2T + [r_h2halo], [], "dbg")

            S.barrier()
            A.off = phase_off

            check_stop("C")
            NW2 = 4
            wbuf2 = [A.bf(2048) for _ in range(NW2)]; r_w2 = [R() for _ in range(NW2)]
            wslab = [A.bf(44 * 128) for _ in range(2)]; r_wslab = [R() for _ in range(2)]
            uT = A.bf(44 * 512); r_uT = [R("uT") for _ in range(44)]
            uT3 = uT.rearrange("p (f t) -> p f t", t=512)
            raw = [A.f32(514) for _ in range(2)]; r_raw = [R() for _ in range(2)]
            ya = A.f32(512); r_ya = R(); yg = A.f32(512); r_yg = R(); sgg = A.f32(512); r_sgg = R()
            ofm = [A.f32(512) for _ in range(2)]; r_ofm = [R() for _ in range(2)]
            res = [A.f32(D) for _ in range(4)]; r_res = [R() for _ in range(4)]
            wi2 = [0]
            ui = [0]

            for blk in range(4):
                t0 = blk * 512
                hregs = [r_h2T[blk * 4 + j] for j in range(4)] + ([r_h2halo] if blk == 0 else [r_h2T[blk * 4 - 1]])
                for jt in range(4):
                    dma_sync(res[jt], y[t0 + jt * 128:t0 + (jt + 1) * 128, :], [r_y[blk * 4 + jt]], [r_res[jt]], f"res{jt}")
                for f in range(44):
                    for half in range(2):
                        tix = f + 44 * half
                        s = wi2[0] % NW2
                        wi2[0] += 1
                        dma_sync(wbuf2[s], wup_all[tix * 128:(tix + 1) * 128, :], [r_wupall], [r_w2[s]], f"w2{s}")
                        w3 = wbuf2[s].rearrange("p (c n) -> p c n", n=128)
                        u = ui[0] % 2
                        ui[0] += 1
                        pb = 2 * u

                        def fn_up(e, w3=w3, pb=pb, t0=t0):
                            ins = None
                            for g in range(2):
                                for c in range(16):
                                    ins = e.matmul(psF[pb + g][:, 0:257], lhsT=w3[:, c, :], rhs=h2T3[:, c, t0 + g * 257:t0 + (g + 1) * 257], start=(c == 0), stop=(c == 15))
                            return ins
                        S.op("pe", fn_up, [r_w2[s]] + hregs, [rF[pb], rF[pb + 1]])
                        rw = raw[u]; r_rw = r_raw[u]
                        S.op("act", lambda e, rw=rw, pb=pb: e.activation(out=rw[:, 0:257], in_=psF[pb][:, 0:257], func=AF.Copy), [rF[pb]], [r_rw])
                        S.op("act", lambda e, rw=rw, pb=pb: e.activation(out=rw[:, 257:514], in_=psF[pb + 1][:, 0:257], func=AF.Copy), [rF[pb + 1]], [r_rw])
                        yy, r_yy = (ya, r_ya) if half == 0 else (yg, r_yg)
                        w0 = cw[:, tix * 3 + 0:tix * 3 + 1]; w1 = cw[:, tix * 3 + 1:tix * 3 + 2]; w2 = cw[:, tix * 3 + 2:tix * 3 + 3]
                        bcol = cb[:, tix:tix + 1]
                        S.op("pool", lambda e, rw=rw, yy=yy, w2=w2, bcol=bcol: e.tensor_scalar(out=yy, in0=rw[:, 2:514], scalar1=w2, scalar2=bcol, op0=ALU.mult, op1=ALU.add), [r_rw, r_small], [r_yy])
                        S.op("dve", lambda e, rw=rw, yy=yy, w1=w1: e.scalar_tensor_tensor(out=yy, in0=rw[:, 1:513], scalar=w1, in1=yy, op0=ALU.mult, op1=ALU.add), [r_rw, r_yy, r_small], [r_yy])
                        S.op("dve", lambda e, rw=rw, yy=yy, w0=w0: e.scalar_tensor_tensor(out=yy, in0=rw[:, 0:512], scalar=w0, in1=yy, op0=ALU.mult, op1=ALU.add), [r_rw, r_yy, r_small], [r_yy])
                    S.op("act", lambda e: e.activation(out=sgg, in_=yg, func=AF.Silu), [r_yg], [r_sgg])
                    S.op("pool", lambda e, f=f: e.tensor_tensor(out=uT3[:, f, :], in0=ya, in1=sgg, op=ALU.mult), [r_ya, r_sgg], [r_uT[f]])
                for n in range(16):
                    sl = n % 2
                    dma_sync(wslab[sl], wdn_all[n * 128:(n + 1) * 128, :], [r_wdnall], [r_wslab[sl]], f"wsl{sl}")
                    ws3 = wslab[sl].rearrange("p (f n) -> p f n", n=128)
                    pd = 4

                    def fn_dn(e, ws3=ws3):
                        ins = None
                        for f in range(44):
                            ins = e.matmul(psF[4], lhsT=ws3[:, f, :], rhs=uT3[:, f, :], start=(f == 0), stop=(f == 43))
                        return ins
                    S.op("pe", fn_dn, [r_wslab[sl]] + r_uT, [rF[4]])
                    S.op("act", lambda e, sl=sl: e.activation(out=ofm[sl], in_=psF[4], func=AF.Copy), [rF[4]], [r_ofm[sl]])

                    def fn_tr(e, sl=sl):
                        ins = None
                        for jt in range(4):
                            ins = e.transpose(out=psF[5][:, jt * 128:(jt + 1) * 128], in_=ofm[sl][:, jt * 128:(jt + 1) * 128], identity=ident_f)
                        return ins
                    S.op("pe", fn_tr, [r_ofm[sl], r_identf], [rF[5]])
                    for jt in range(4):
                        S.op("dve", lambda e, jt=jt, n=n: e.tensor_tensor(out=res[jt][:, n * 128:(n + 1) * 128], in0=psF[5][:, jt * 128:(jt + 1) * 128], in1=res[jt][:, n * 128:(n + 1) * 128], op=ALU.add), [rF[5], r_res[jt]], [r_res[jt]])
                for jt in range(4):
                    dma_sync(y[t0 + jt * 128:t0 + (jt + 1) * 128, :], res[jt], [r_res[jt]], [r_y[blk * 4 + jt]], f"yo{jt}")
        except StopBuild:
            pass

        keys = list(COMPUTE) + list(S.dcnt.keys())
        sems = {k: es.enter_context(nc.semaphore(f"s_{k}")) for k in keys}
        with nc.Block() as block:
            @block.sync
            def _(e):
                S.replay("sync", e, sems)

            @block.scalar
            def _(e):
                S.replay("act", e, sems)

            @block.vector
            def _(e):
                S.replay("dve", e, sems)

            @block.gpsimd
            def _(e):
                S.replay("pool", e, sems)

            @block.tensor
            def _(e):
                S.replay("pe", e, sems)
    return nc


def make_inputs(x, g_mix, w_in, hg_lb, hg_out_gain, q_gain, k_gain, rel_bias,
                at_out_gain, w_out, g_ffn, w_up, conv_w, conv_b, w_down):
    f32 = np.float32
    x = np.asarray(x, f32)[0]
    xpad = np.concatenate([np.zeros((HAL, D), f32), x], axis=0)
    shared = {}
    shared["gmix_b"] = np.ascontiguousarray(np.broadcast_to(np.asarray(g_mix, f32)[0][None, :], (128, D)))
    shared["gffn_b"] = np.ascontiguousarray(np.broadcast_to(np.asarray(g_ffn, f32)[0][None, :], (128, D)))
    wi = np.asarray(w_in, f32)[0]
    shared["w_in_t"] = np.ascontiguousarray(wi.reshape(16, 128, 56, 128).transpose(2, 1, 0, 3)).reshape(56, 128, 2048)
    wo = np.asarray(w_out, f32)[0]
    shared["w_out_t"] = np.ascontiguousarray(wo.reshape(16, 128, 2048).transpose(1, 0, 2)).reshape(128, 16 * 2048)
    wu = np.asarray(w_up, f32)[0]
    w_up_t = np.ascontiguousarray(wu.reshape(16, 128, 88, 128).transpose(2, 1, 0, 3)).reshape(88 * 128, 2048)
    wd = np.asarray(w_down, f32)[0]
    w_down_t = np.ascontiguousarray(wd.reshape(44, 128, 16, 128).transpose(2, 1, 0, 3)).reshape(16 * 128, 44 * 128)
    lbr = np.asarray(hg_lb, f32).reshape(2, 8, 128).transpose(2, 0, 1).reshape(128, 16)
    shared["lbraw"] = np.ascontiguousarray(lbr)
    shared["hgain_c"] = np.ascontiguousarray(np.asarray(hg_out_gain, f32)[0].reshape(8, 128).T)
    shared["atgain_c"] = np.ascontiguousarray(np.asarray(at_out_gain, f32)[0].reshape(8, 128).T)
    shared["qkg_c"] = np.ascontiguousarray(np.stack([np.asarray(q_gain, f32)[0], np.asarray(k_gain, f32)[0]], axis=1))
    rb = np.asarray(rel_bias, f32)[0]
    t = np.arange(64)
    u = np.arange(576)
    dist = 512 + t[:, None] - u[None, :]
    ridx = np.clip(dist, -128, 128) + 128
    bias = rb[:, ridx]
    relb = np.full((8, 128, 640), NEG, f32)
    relb[:, 0:64, 0:576] = bias
    relb[:, 64:128, 64:640] = bias
    shared["relb"] = relb
    shared["ident"] = np.eye(128, dtype=f32)
    ii = np.arange(128)
    tri = ((ii[:, None] <= ii[None, :]) & ((ii[:, None] // 64) == (ii[None, :] // 64))).astype(f32)
    shared["tri4"] = np.ascontiguousarray(np.tile(tri, (1, 4)))
    seg = np.ones((128, 512), f32)
    seg[:, ::64] = 0.0
    shared["segmask"] = seg
    cwv = np.asarray(conv_w, f32)[0]
    shared["cw_c"] = np.ascontiguousarray(cwv.reshape(3, 88, 128).transpose(2, 1, 0)).reshape(128, 88 * 3)
    shared["cb_c"] = np.ascontiguousarray(np.asarray(conv_b, f32)[0].reshape(88, 128).T)
    in_maps = []
    for c in range(NCORES):
        m = dict(shared)
        m["xh"] = np.ascontiguousarray(xpad[c * TOK:c * TOK + TOK + HAL])
        m["w_up_sh"] = np.ascontiguousarray(w_up_t[c * 1408:(c + 1) * 1408])
        m["w_down_sh"] = np.ascontiguousarray(w_down_t[c * 256:(c + 1) * 256])
        hm = np.zeros((128, 512), f32)
        if c == 0:
            hm[:] = NEG
        m["hmask"] = hm
        sel = np.zeros((128, 32), f32)
        if c > 0:
            sel[2 * (c - 1), 0] = 1.0
            sel[2 * (c - 1) + 1, 1] = 1.0
        m["sel"] = sel
        al = np.zeros((128, 16), f32)
        al[:, 0:c] = 1.0
        al[:, 8:] = 0.0
        m["alpha"] = al
        in_maps.append(m)
    return in_maps


_NC_CACHE = {}


def kernel(**inputs):
    debug = bool(os.environ.get("MK_DEBUG"))
    in_maps = make_inputs(**inputs)
    if debug not in _NC_CACHE:
        _NC_CACHE[debug] = build_nc(debug)
    nc = _NC_CACHE[debug]
    res = run_bass_kernel_spmd(nc, in_maps, core_ids=list(range(NCORES)))
    out = np.concatenate([np.asarray(r["y"], np.float32) for r in res.results], axis=0)
    if debug:
        kernel.last = res
    return out[None, :, :]
```
